# Optimizing a Trainium2 kernel written in Bass

```python
import jax, jax.numpy as jnp
from jax import lax
import numpy as np

D_MODEL = 2048
BATCH = 1
SEQ = 8192
DEPTH = 1
DEC_BATCH = 16
DEC_SEQ = 64
PAST_LEN = 4096

CHUNK = 64
H_RET = 8
DK_RET = 128
DV_RET = 128
H_GDN = 8
DK_GDN = 128
DV_GDN = 128
CONV_W = 4
D_FF = 5632
ROPE_BASE = 10000.0
EPS = 1e-6

RET_QK = H_RET * DK_RET
RET_V = H_RET * DV_RET
GDN_QK = H_GDN * DK_GDN
GDN_V = H_GDN * DV_GDN
GDN_CONV_CH = 2 * GDN_QK + GDN_V
D_MIX = RET_V + GDN_V
IN_SIZES = [RET_QK, RET_QK, RET_V, RET_V, GDN_CONV_CH, GDN_V, H_GDN, H_GDN]
IN_SPLITS = [int(s) for s in np.cumsum(IN_SIZES)[:-1]]
N_IN = int(sum(IN_SIZES))

kernel_name = "hybrid_retention_gdn_macaron_step"


def _rmsnorm(x, g):
    xf = x.astype(jnp.float32)
    y = xf * lax.rsqrt(jnp.mean(xf * xf, axis=-1, keepdims=True) + EPS)
    return (y * g.astype(jnp.float32)).astype(x.dtype)


def _head_rmsnorm(o, g):
    return o * lax.rsqrt(jnp.mean(o * o, axis=-1, keepdims=True) + EPS) * g.astype(jnp.float32)


def _l2norm(x):
    return x * lax.rsqrt(jnp.sum(x * x, axis=-1, keepdims=True) + EPS)


def _swiglu(h, w_gate, w_up, w_down):
    return (jax.nn.silu(h @ w_gate) * (h @ w_up)) @ w_down


def _rotary(x, pos):
    half = x.shape[-1] // 2
    inv = ROPE_BASE ** (-jnp.arange(half, dtype=jnp.float32) / half)
    ang = pos[:, None] * inv[None, :]
    cos = jnp.cos(ang)[None, :, None, :]
    sin = jnp.sin(ang)[None, :, None, :]
    x1, x2 = x[..., :half], x[..., half:]
    return jnp.concatenate([x1 * cos - x2 * sin, x1 * sin + x2 * cos], axis=-1)


def _retention(q, k, v, s0, log_gamma):
    B, L, H, DK = q.shape
    DV = v.shape[-1]
    c = min(CHUNK, L)
    n = L // c
    qc = q.reshape(B, n, c, H, DK)
    kc = k.reshape(B, n, c, H, DK)
    vc = v.reshape(B, n, c, H, DV)
    t = jnp.arange(c, dtype=jnp.float32)
    diff = t[:, None] - t[None, :]
    causal = diff >= 0
    dmat = jnp.where(causal[None], jnp.exp(jnp.where(causal, diff, 0.0)[None] * log_gamma[:, None, None]), 0.0)
    scores = jnp.einsum('bnthd,bnshd->bnhts', qc, kc) * dmat
    o_intra = jnp.einsum('bnhts,bnshe->bnthe', scores, vc)
    q_dec = jnp.exp((t + 1.0)[:, None] * log_gamma[None, :])
    k_dec = jnp.exp((c - 1.0 - t)[:, None] * log_gamma[None, :])
    chunk_dec = jnp.exp(c * log_gamma)
    kv = jnp.einsum('bnshd,bnshe->bnhde', kc * k_dec[None, None, :, :, None], vc)

    def step(s, kv_i):
        return s * chunk_dec[None, :, None, None] + kv_i, s

    s_final, s_prev = lax.scan(step, s0, jnp.moveaxis(kv, 1, 0))
    s_prev = jnp.moveaxis(s_prev, 0, 1)
    o_inter = jnp.einsum('bnthd,bnhde->bnthe', qc * q_dec[None, None, :, :, None], s_prev)
    return (o_intra + o_inter).reshape(B, L, H, DV), s_final


def _gated_delta(q, k, v, g, beta, s0):
    B, L, H, DK = q.shape
    DV = v.shape[-1]
    c = min(CHUNK, L)
    n = L // c
    qc = q.reshape(B, n, c, H, DK).transpose(0, 1, 3, 2, 4)
    kc = k.reshape(B, n, c, H, DK).transpose(0, 1, 3, 2, 4)
    vc = v.reshape(B, n, c, H, DV).transpose(0, 1, 3, 2, 4)
    gc = jnp.cumsum(g.reshape(B, n, c, H).transpose(0, 1, 3, 2), axis=-1)
    bc = beta.reshape(B, n, c, H).transpose(0, 1, 3, 2)
    t = jnp.arange(c)
    tri = t[:, None] >= t[None, :]
    strict = t[:, None] > t[None, :]
    gdiff = gc[..., :, None] - gc[..., None, :]
    decay = jnp.where(tri, jnp.exp(jnp.where(tri, gdiff, 0.0)), 0.0)
    kk = jnp.einsum('bnhtd,bnhsd->bnhts', kc, kc)
    a_mat = jnp.where(strict, kk * decay * bc[..., :, None], 0.0)
    lhs = a_mat + jnp.eye(c, dtype=jnp.float32)
    w = lax.linalg.triangular_solve(lhs, kc * (bc * jnp.exp(gc))[..., None], left_side=True, lower=True, unit_diagonal=True)
    u = lax.linalg.triangular_solve(lhs, vc * bc[..., None], left_side=True, lower=True, unit_diagonal=True)
    qk = jnp.einsum('bnhtd,bnhsd->bnhts', qc, kc) * decay
    q_dec = qc * jnp.exp(gc)[..., None]
    k_dec = kc * jnp.exp(gc[..., -1:] - gc)[..., None]
    chunk_dec = jnp.exp(gc[..., -1])

    def step(s, inp):
        w_i, u_i, qk_i, q_i, kd_i, cd_i = inp
        delta = u_i - jnp.einsum('bhtd,bhde->bhte', w_i, s)
        o = jnp.einsum('bhtd,bhde->bhte', q_i, s) + jnp.einsum('bhts,bhse->bhte', qk_i, delta)
        s_new = s * cd_i[..., None, None] + jnp.einsum('bhtd,bhte->bhde', kd_i, delta)
        return s_new, o

    xs = tuple(jnp.moveaxis(a, 1, 0) for a in (w, u, qk, q_dec, k_dec, chunk_dec))
    s_final, o = lax.scan(step, s0, xs)
    o = o.transpose(1, 0, 3, 2, 4).reshape(B, L, H, DV)
    return o, s_final


def _mixer(h, s_ret, s_gdn, conv_buf, pos, w_in, ret_norm, conv_w, a_log, dt_bias, gdn_norm, w_out):
    B, L, _ = h.shape
    f32 = jnp.float32
    p = (h @ w_in).astype(f32)
    rq, rk, rv, rg, gqkv, gg, ga, gb = jnp.split(p, IN_SPLITS, axis=-1)
    log_gamma = jnp.log(1.0 - 2.0 ** (-5.0 - jnp.arange(H_RET, dtype=f32)))
    rq = _rotary(rq.reshape(B, L, H_RET, DK_RET), pos) * (DK_RET ** -0.5)
    rk = _rotary(rk.reshape(B, L, H_RET, DK_RET), pos)
    o_r, s_ret_new = _retention(rq, rk, rv.reshape(B, L, H_RET, DV_RET), s_ret.astype(f32), log_gamma)
    o_r = _head_rmsnorm(o_r, ret_norm.reshape(H_RET, DV_RET)) * jax.nn.silu(rg).reshape(B, L, H_RET, DV_RET)
    xpad = jnp.concatenate([conv_buf.astype(f32), gqkv], axis=1)
    conv_new = xpad[:, L:]
    cw = conv_w.astype(f32)
    gqkv = jax.nn.silu(sum(xpad[:, i:i + L] * cw[i] for i in range(CONV_W)))
    gq, gk, gv = jnp.split(gqkv, [GDN_QK, 2 * GDN_QK], axis=-1)
    gq = _l2norm(gq.reshape(B, L, H_GDN, DK_GDN)) * (DK_GDN ** -0.5)
    gk = _l2norm(gk.reshape(B, L, H_GDN, DK_GDN))
    gv = gv.reshape(B, L, H_GDN, DV_GDN)
    g_log = -jnp.exp(a_log.astype(f32)) * jax.nn.softplus(ga + dt_bias.astype(f32))
    beta = jax.nn.sigmoid(gb)
    o_g, s_gdn_new = _gated_delta(gq, gk, gv, g_log, beta, s_gdn.astype(f32))
    o_g = _head_rmsnorm(o_g, gdn_norm) * jax.nn.silu(gg).reshape(B, L, H_GDN, DV_GDN)
    o = jnp.concatenate([o_r.reshape(B, L, RET_V), o_g.reshape(B, L, GDN_V)], axis=-1).astype(h.dtype)
    return o @ w_out, s_ret_new.astype(s_ret.dtype), s_gdn_new.astype(s_gdn.dtype), conv_new.astype(conv_buf.dtype)


def _layer(x, s_ret, s_gdn, conv_buf, pos, lp):
    (ffn1_norm, ffn1_w_gate, ffn1_w_up, ffn1_w_down, mix_norm, w_in, ret_norm, gdn_conv,
     gdn_a_log, gdn_dt_bias, gdn_norm, w_out, ffn2_norm, ffn2_w_gate, ffn2_w_up, ffn2_w_down) = lp
    x = x + 0.5 * _swiglu(_rmsnorm(x, ffn1_norm), ffn1_w_gate, ffn1_w_up, ffn1_w_down)
    m, s_ret, s_gdn, conv_buf = _mixer(_rmsnorm(x, mix_norm), s_ret, s_gdn, conv_buf, pos, w_in, ret_norm,
                                       gdn_conv, gdn_a_log, gdn_dt_bias, gdn_norm, w_out)
    x = x + m
    x = x + 0.5 * _swiglu(_rmsnorm(x, ffn2_norm), ffn2_w_gate, ffn2_w_up, ffn2_w_down)
    return x, s_ret, s_gdn, conv_buf


def setup_inputs(seed: int = 0) -> dict:
    key = jax.random.key(seed)
    ks = jax.random.split(key, 24)
    f32 = jnp.float32

    def nrm(k, shape, scale):
        return jax.random.normal(k, shape, f32) * scale

    def gain(k, shape):
        return 1.0 + 0.02 * jax.random.normal(k, shape, f32)

    dt = jnp.exp(jax.random.uniform(ks[14], (DEPTH, H_GDN), f32, np.log(1e-3), np.log(1e-1)))
    return {
        "x_prompt": nrm(ks[0], (BATCH, SEQ, D_MODEL), 1.0),
        "x_sample": nrm(ks[1], (DEC_BATCH, DEC_SEQ, D_MODEL), 1.0),
        "state_ret": nrm(ks[2], (DEPTH, DEC_BATCH, H_RET, DK_RET, DV_RET), 0.1),
        "state_gdn": nrm(ks[3], (DEPTH, DEC_BATCH, H_GDN, DK_GDN, DV_GDN), 0.1),
        "state_conv": nrm(ks[4], (DEPTH, DEC_BATCH, CONV_W - 1, GDN_CONV_CH), 1.0),
        "ffn1_norm": gain(ks[5], (DEPTH, D_MODEL)),
        "ffn1_w_gate": nrm(ks[6], (DEPTH, D_MODEL, D_FF), D_MODEL ** -0.5),
        "ffn1_w_up": nrm(ks[7], (DEPTH, D_MODEL, D_FF), D_MODEL ** -0.5),
        "ffn1_w_down": nrm(ks[8], (DEPTH, D_FF, D_MODEL), D_FF ** -0.5),
        "mix_norm": gain(ks[9], (DEPTH, D_MODEL)),
        "w_in": nrm(ks[10], (DEPTH, D_MODEL, N_IN), D_MODEL ** -0.5),
        "ret_norm": gain(ks[11], (DEPTH, RET_V)),
        "gdn_conv": nrm(ks[12], (DEPTH, CONV_W, GDN_CONV_CH), CONV_W ** -0.5),
        "gdn_a_log": jnp.log(jax.random.uniform(ks[13], (DEPTH, H_GDN), f32, 1.0, 16.0)),
        "gdn_dt_bias": dt + jnp.log(-jnp.expm1(-dt)),
        "gdn_norm": gain(ks[15], (DEPTH, DV_GDN)),
        "w_out": nrm(ks[16], (DEPTH, D_MIX, D_MODEL), D_MIX ** -0.5),
        "ffn2_norm": gain(ks[17], (DEPTH, D_MODEL)),
        "ffn2_w_gate": nrm(ks[18], (DEPTH, D_MODEL, D_FF), D_MODEL ** -0.5),
        "ffn2_w_up": nrm(ks[19], (DEPTH, D_MODEL, D_FF), D_MODEL ** -0.5),
        "ffn2_w_down": nrm(ks[20], (DEPTH, D_FF, D_MODEL), D_FF ** -0.5),
        "final_norm": gain(ks[21], (D_MODEL,)),
    }


def reference(x_prompt, x_sample, state_ret, state_gdn, state_conv, ffn1_norm, ffn1_w_gate, ffn1_w_up,
              ffn1_w_down, mix_norm, w_in, ret_norm, gdn_conv, gdn_a_log, gdn_dt_bias, gdn_norm, w_out,
              ffn2_norm, ffn2_w_gate, ffn2_w_up, ffn2_w_down, final_norm):
    Bp, Lp, _ = x_prompt.shape
    Ls = x_sample.shape[1]
    pos_p = jnp.arange(Lp, dtype=jnp.float32)
    pos_s = PAST_LEN + jnp.arange(Ls, dtype=jnp.float32)
    yp, ys = x_prompt, x_sample
    rp, gp, cp, rs, gs, cs = [], [], [], [], [], []
    for l in range(DEPTH):
        lp = (ffn1_norm[l], ffn1_w_gate[l], ffn1_w_up[l], ffn1_w_down[l], mix_norm[l], w_in[l], ret_norm[l],
              gdn_conv[l], gdn_a_log[l], gdn_dt_bias[l], gdn_norm[l], w_out[l], ffn2_norm[l], ffn2_w_gate[l],
              ffn2_w_up[l], ffn2_w_down[l])
        z_ret = jnp.zeros((Bp, H_RET, DK_RET, DV_RET), state_ret.dtype)
        z_gdn = jnp.zeros((Bp, H_GDN, DK_GDN, DV_GDN), state_gdn.dtype)
        z_conv = jnp.zeros((Bp, CONV_W - 1, GDN_CONV_CH), state_conv.dtype)
        yp, a, b, c = _layer(yp, z_ret, z_gdn, z_conv, pos_p, lp)
        ys, d, e, f = _layer(ys, state_ret[l], state_gdn[l], state_conv[l], pos_s, lp)
        rp.append(a); gp.append(b); cp.append(c)
        rs.append(d); gs.append(e); cs.append(f)
    y_prompt = _rmsnorm(yp, final_norm)
    y_sample = _rmsnorm(ys, final_norm)
    return (y_prompt, y_sample, jnp.stack(rp), jnp.stack(gp), jnp.stack(cp), jnp.stack(rs), jnp.stack(gs), jnp.stack(cs))
```

```python
import numpy as np
import ml_dtypes
import concourse.bass as bass
import concourse.mybir as mybir
from concourse.bass_utils import run_bass_kernel_spmd

F32 = mybir.dt.float32
BF16 = mybir.dt.bfloat16
I32 = mybir.dt.int32
AF = mybir.ActivationFunctionType
ALU = mybir.AluOpType
AX = mybir.AxisListType

NCORES = 8
D = 2048
KD = 16
DFF = 5632
NFF = 44
TOK = 1152
NB = 3
BW = 384
EPS = 1e-6
NCOLW = 1282
NPOS = 8192 + 512
NEG = -30000.0
GRP_FF = 4
DEBUG = False
DBG_GI = 16
SKIP_MIXER = False
CSTW = 520


class Res:
    __slots__ = ("name", "lw", "rd", "ds")

    def __init__(self, name):
        self.name = name
        self.lw = None
        self.rd = {}
        self.ds = None


class KB:
    def __init__(self, nc):
        self.nc = nc
        self.eng = {"pe": nc.tensor, "act": nc.scalar, "dve": nc.vector, "pool": nc.gpsimd, "sp": nc.sync}
        self.sems = {}
        self.cnt = {}
        self.waited = {}
        for e in self.eng:
            self.sems[e] = nc.semaphore("prog_" + e).__enter__()
            self.cnt[e] = 0
        self.ndsem = 0

    def dsem(self, name=None):
        self.ndsem += 1
        key = "d%d_%s" % (self.ndsem, name or "")
        self.sems[key] = self.nc.semaphore(key).__enter__()
        self.cnt[key] = 0
        return key

    def _wait(self, e, key, val, prod):
        if prod == e and key == e and e == "pe":
            return
        if self.waited.get((e, key), 0) >= val:
            return
        self.eng[e].wait_ge(self.sems[key], val)
        self.waited[(e, key)] = val

    def _deps(self, e, r, w, is_dma=False):
        for res in r:
            if res.lw is not None:
                self._wait(e, *res.lw)
        for res in w:
            if res.lw is not None:
                self._wait(e, *res.lw)
            for key, (val, prod) in res.rd.items():
                self._wait(e, key, val, prod)

    def _commit(self, ev, r, w):
        for res in w:
            res.lw = ev
            res.rd = {}
        for res in r:
            res.rd[ev[0]] = (ev[1], ev[2])

    def op(self, e, fn, r=(), w=()):
        self._deps(e, r, w)
        inst = fn(self.eng[e])
        self.cnt[e] += 1
        inst.then_inc(self.sems[e], 1)
        self._commit((e, self.cnt[e], e), r, w)

    def _ds_of(self, on):
        if isinstance(on, str):
            return on
        if on.ds is None:
            on.ds = self.dsem(on.name)
        return on.ds

    def dma(self, q, on, out, in_, r=(), w=()):
        ds = self._ds_of(on)
        self._deps(q, r, w, True)
        self.eng[q].dma_start(out=out, in_=in_).then_inc(self.sems[ds], 16)
        self.cnt[ds] += 16
        self._commit((ds, self.cnt[ds], None), r, w)

    def idma(self, on, out, in_, idx_ap, r=(), w=()):
        ds = self._ds_of(on)
        self._deps("pool", r, w, True)
        self.nc.gpsimd.indirect_dma_start(
            out=out, out_offset=None, in_=in_,
            in_offset=bass.IndirectOffsetOnAxis(ap=idx_ap, axis=0)).then_inc(self.sems[ds], 16)
        self.cnt[ds] += 16
        self._commit((ds, self.cnt[ds], None), r, w)

    def allgather(self, ds, in_ap, out_ap, r=(), w=()):
        self._deps("pool", r, w, True)
        self.nc.gpsimd.collective_compute(
            "AllGather", ALU.bypass, replica_groups=[list(range(NCORES))],
            ins=[in_ap], outs=[out_ap]).then_inc(self.sems[ds], 1)
        self.cnt[ds] += 1
        self._commit((ds, self.cnt[ds], None), r, w)

    def barrier(self, skip=()):
        for e in self.eng:
            for key in self.sems:
                if key != e and key not in skip and self.cnt[key] > 0:
                    self._wait(e, key, self.cnt[key], None)

    def final_wait(self, e="sp"):
        for key in self.sems:
            if key != e and self.cnt[key] > 0:
                self._wait(e, key, self.cnt[key], None)


def bc(ap, shape, axis):
    return ap.unsqueeze(axis).to_broadcast(shape)


def build_program():
    from contextlib import ExitStack
    nc = bass.Bass("TRN2", target_bir_lowering=False)
    K = KB(nc)

    def din(name, shape, dt=F32):
        return nc.dram_tensor(name, shape, dt, kind="ExternalInput").ap()

    def dout(name, shape, dt=F32):
        return nc.dram_tensor(name, shape, dt, kind="ExternalOutput").ap()

    x_tok = din("x_tok", [TOK, D])
    nrm_d = din("nrm", [128, 4 * KD])
    wg_d = [din("wg1", [D, DFF]), din("wg2", [D, DFF])]
    wu_d = [din("wu1", [D, DFF]), din("wu2", [D, DFF])]
    wd_d = [din("wd1", [DFF, D]), din("wd2", [DFF, D])]
    win_d = din("w_in_c", [D, NCOLW])
    wout_d = din("w_out", [D, D])
    cst_d = din("cst", [128, CSTW])
    tabs_d = din("tabs", [128, 2, NPOS])
    sret_d = din("s_ret", [16, 128, 128])
    sgdn_d = din("s_gdn", [16, 128, 128])
    sconv_d = din("s_conv", [128, 3, 16, 3])
    hp_d = din("hp", [128, 16])
    gidx_d = din("gidx", [128, KD], I32)

    y_d = dout("y", [TOK, D])
    oret_d = dout("st_ret", [17, 128, 128])
    ogdn_d = dout("st_gdn", [17, 128, 128])
    oconv_d = dout("st_conv", [128, 3, 17, 3])

    h_in = nc.dram_tensor("h_ag_in", [D, TOK], BF16)
    h_all = nc.dram_tensor("h_ag_out", [NCORES * D, TOK], BF16)
    o_in = nc.dram_tensor("o_ag_in", [NCORES * 256, TOK], BF16)
    o_all = nc.dram_tensor("o_ag_out", [NCORES * NCORES * 256, TOK], BF16)
    if DEBUG:
        x1_sp = nc.dram_tensor("x1_spill", [128, KD * TOK], F32, kind="ExternalOutput")
        dbg_h = dout("dbg_h", [NCORES * D, TOK], BF16)
        dbg_o = dout("dbg_o", [NCORES * 256, TOK], BF16)
    else:
        x1_sp = nc.dram_tensor("x1_spill", [128, KD * TOK], F32)
    R_hin, R_hall, R_oin, R_oall, R_x1sp = Res("hin"), Res("hall"), Res("oin"), Res("oall"), Res("x1sp")

    cst = nc.alloc_sbuf_tensor("cst_sb", [128, CSTW], F32)
    nrm = nc.alloc_sbuf_tensor("nrm_sb", [128, 4 * KD], F32)
    hp = nc.alloc_sbuf_tensor("hp_sb", [128, 16], F32)
    hpx = nc.alloc_sbuf_tensor("hpx", [128, 8], F32)
    gidx = nc.alloc_sbuf_tensor("gidx_sb", [128, KD], I32)
    R_c = Res("consts")
    K.dma("sp", R_c, cst[:], cst_d, w=[R_c])
    K.dma("sp", R_c, nrm[:], nrm_d, w=[R_c])
    K.dma("sp", R_c, hp[:], hp_d, w=[R_c])
    K.dma("sp", R_c, gidx[:], gidx_d, w=[R_c])
    ident = cst[:, 0:128]
    ones = cst[:, 128:256]
    maskR = cst[0:64, 256:320]
    Umat = cst[0:64, 320:384]
    NEGs = cst[0:64, 384:448]
    NEGu = cst[0:64, 448:512]
    c_a = cst[0:64, 512:513]
    c_a2 = cst[0:64, 513:514]
    c_kd = cst[0:64, 514:515]
    c_g64 = cst[:, 515:516]
    K.op("dve", lambda e: e.tensor_scalar(out=nrm[:], in0=nrm[:], scalar1=float(np.sqrt(D)), scalar2=None,
                                          op0=ALU.mult), r=[], w=[R_c])
    K.op("act", lambda e: e.activation(out=hpx[:, 0:1], in_=hp[:, 2:3], func=AF.Exp), r=[R_c], w=[R_c])
    K.op("dve", lambda e: e.tensor_scalar(out=hpx[:, 0:1], in0=hpx[:, 0:1], scalar1=-1.0, scalar2=None,
                                          op0=ALU.mult), r=[], w=[R_c])
    negA = hpx[0:64, 0:1]
    K.op("pool", lambda e: e.memset(hpx[:, 1:2], float(EPS)), r=[], w=[R_c])
    K.op("pool", lambda e: e.memset(hpx[:, 2:3], float(128 * EPS)), r=[], w=[R_c])

    def eps_ap(P):
        return hpx[0:P, 1:2]

    PS = [nc.alloc_psum_tensor("ps%d" % i, [128, 512], F32) for i in range(8)]
    RPS = [Res("ps%d" % i) for i in range(8)]
    ds_x = K.dsem("xio")
    ds_cc = K.dsem("cc")

    def sb(stack, name, shape, dt=F32):
        return stack.enter_context(nc.sbuf_tensor(name, shape, dt))

    def rmsnorm_fm(xT, RxT, out_fn, sq, Rsq, rs, Rrs):
        for nb in range(NB):
            cols = slice(nb * BW, (nb + 1) * BW)
            bank = nb % 2
            for k4 in range(4):
                K.op("act", lambda e: e.activation(out=sq[:, :, :], in_=xT[:, k4 * 4:(k4 + 1) * 4, cols],
                                                   func=AF.Square), r=[RxT], w=[Rsq])
                for kk in range(4):
                    k = k4 * 4 + kk
                    K.op("pe", lambda e: e.matmul(PS[bank][:, 0:BW], ones, sq[:, kk, :], start=(k == 0),
                                                  stop=(k == KD - 1)), r=[Rsq, R_c], w=[RPS[bank]])
            K.op("dve", lambda e: e.tensor_scalar(out=rs[:, :], in0=PS[bank][:, 0:BW], scalar1=float(D * EPS),
                                                  scalar2=None, op0=ALU.add), r=[RPS[bank]], w=[Rrs])
            K.op("act", lambda e: e.activation(out=rs[:, :], in_=rs[:, :], func=AF.Sqrt), r=[], w=[Rrs])
            K.op("dve", lambda e: e.reciprocal(out=rs[:, :], in_=rs[:, :]), r=[], w=[Rrs])
            for k in range(KD):
                out_fn(nb, k, cols)

    def ffn(li, xT, RxT, hT, RhT, stack):
        aT = [sb(stack, "aT%d_%d" % (li, i), [128, GRP_FF, TOK], BF16) for i in range(2)]
        RaT = [Res("aT0"), Res("aT1")]
        wgu = [sb(stack, "wgu%d_%d" % (li, i), [128, 2, KD, 256], BF16) for i in range(2)]
        Rwgu = [Res("wgu0"), Res("wgu1")]
        wdn = [sb(stack, "wdn%d_%d" % (li, i), [128, D], BF16) for i in range(6)]
        Rwdn = [Res("wdn%d" % i) for i in range(6)]
        sg = sb(stack, "sg%d" % li, [128, BW], F32)
        Rsg = Res("sg")
        wg_v = wg_d[li].rearrange("(k p) f -> p k f", p=128)
        wu_v = wu_d[li].rearrange("(k p) f -> p k f", p=128)
        ngrp = NFF // GRP_FF
        st = {"pair": 0, "dn": 0}

        def load_gu(pair):
            s = pair % 2
            fc = slice(pair * 256, (pair + 1) * 256)
            K.dma("pool", Rwgu[s], wgu[s][:, 0, :, :], wg_v[:, :, fc], w=[Rwgu[s]])
            K.dma("pool", Rwgu[s], wgu[s][:, 1, :, :], wu_v[:, :, fc], w=[Rwgu[s]])

        def load_wd(f):
            s = f % 6
            K.dma("pool", Rwdn[s], wdn[s][:, :], wd_d[li][f * 128:(f + 1) * 128, :], w=[Rwdn[s]])

        def gate_up(g):
            ab = g % 2
            for fl in range(GRP_FF):
                f = g * GRP_FF + fl
                pair, s, off = f // 2, (f // 2) % 2, (f % 2) * 128
                for nb in range(NB):
                    cols = slice(nb * BW, (nb + 1) * BW)
                    pg, pu = 2 * (st["pair"] % 3), 2 * (st["pair"] % 3) + 1
                    st["pair"] += 1
                    for k in range(KD):
                        K.op("pe", lambda e: e.matmul(PS[pg][:, 0:BW], wgu[s][:, 0, k, off:off + 128], hT[:, k, cols],
                                                      start=(k == 0), stop=(k == KD - 1)),
                             r=[Rwgu[s], RhT], w=[RPS[pg]])
                    for k in range(KD):
                        K.op("pe", lambda e: e.matmul(PS[pu][:, 0:BW], wgu[s][:, 1, k, off:off + 128], hT[:, k, cols],
                                                      start=(k == 0), stop=(k == KD - 1)),
                             r=[Rwgu[s], RhT], w=[RPS[pu]])
                    K.op("act", lambda e: e.activation(out=sg[:, :], in_=PS[pg][:, 0:BW], func=AF.Silu),
                         r=[RPS[pg]], w=[Rsg])
                    K.op("dve", lambda e: e.tensor_tensor(out=aT[ab][:, fl, cols], in0=sg[:, :], in1=PS[pu][:, 0:BW],
                                                          op=ALU.mult), r=[Rsg, RPS[pu]], w=[RaT[ab]])
                if f % 2 == 1 and pair + 2 < NFF // 2:
                    load_gu(pair + 2)

        def down(g):
            ab = g % 2
            for m in range(KD):
                for nb in range(NB):
                    cols = slice(nb * BW, (nb + 1) * BW)
                    pb = 6 + (st["dn"] % 2)
                    st["dn"] += 1
                    for fl in range(GRP_FF):
                        s = (g * GRP_FF + fl) % 6
                        K.op("pe", lambda e: e.matmul(PS[pb][:, 0:BW], wdn[s][:, m * 128:(m + 1) * 128],
                                                      aT[ab][:, fl, cols], start=(fl == 0), stop=(fl == GRP_FF - 1)),
                             r=[Rwdn[s], RaT[ab]], w=[RPS[pb]])
                    K.op("dve", lambda e: e.scalar_tensor_tensor(out=xT[:, m, cols], in0=PS[pb][:, 0:BW], scalar=0.5,
                                                                 in1=xT[:, m, cols], op0=ALU.mult, op1=ALU.add),
                         r=[RPS[pb]], w=[RxT])

        load_gu(0)
        load_gu(1)
        for f in range(6):
            load_wd(f)
        nxt_wd = 6
        for g in range(ngrp + 1):
            if g < ngrp:
                gate_up(g)
            if g >= 1:
                down(g - 1)
                for _ in range(GRP_FF):
                    if nxt_wd < NFF:
                        load_wd(nxt_wd)
                        nxt_wd += 1

    with ExitStack() as ph1:
        xT = sb(ph1, "xT", [128, KD, TOK], F32)
        hT = sb(ph1, "hT", [128, KD, TOK], BF16)
        sq = sb(ph1, "sq", [128, 4, BW], F32)
        rs = sb(ph1, "rs", [128, BW], F32)
        RxT, RhT, Rsq, Rrs = Res("xT"), Res("hT"), Res("sq"), Res("rs")
        with ExitStack() as s1:
            stg = [sb(s1, "stg%d" % i, [128, D], F32) for i in range(2)]
            Rstg = [Res("stg0"), Res("stg1")]
            for i in range(TOK // 128):
                s = i % 2
                K.dma("sp", Rstg[s], stg[s][:, :], x_tok[i * 128:(i + 1) * 128, :], w=[Rstg[s]])
                for k4 in range(4):
                    b = k4 % 2
                    for kk in range(4):
                        k = k4 * 4 + kk
                        K.op("pe", lambda e: e.transpose(PS[b][:, kk * 128:(kk + 1) * 128],
                                                         stg[s][:, k * 128:(k + 1) * 128], ident),
                             r=[Rstg[s], R_c], w=[RPS[b]])
                    dst = xT[:, k4 * 4:(k4 + 1) * 4, i * 128:(i + 1) * 128]
                    srcp = PS[b][:, :].rearrange("p (a t) -> p a t", a=4)
                    if k4 % 2 == 0:
                        K.op("act", lambda e: e.activation(out=dst, in_=srcp, func=AF.Copy), r=[RPS[b]], w=[RxT])
                    else:
                        K.op("dve", lambda e: e.tensor_copy(out=dst, in_=srcp), r=[RPS[b]], w=[RxT])
            K.barrier()

        def norm_to_h(gi):
            def out_fn(nb, k, cols):
                K.op("dve", lambda e: e.scalar_tensor_tensor(out=hT[:, k, cols], in0=xT[:, k, cols],
                                                             scalar=nrm[:, gi * KD + k:gi * KD + k + 1], in1=rs[:, :],
                                                             op0=ALU.mult, op1=ALU.mult), r=[RxT, Rrs, R_c], w=[RhT])
            rmsnorm_fm(xT, RxT, out_fn, sq, Rsq, rs, Rrs)

        norm_to_h(0)
        with ExitStack() as s2:
            ffn(0, xT, RxT, hT, RhT, s2)
            K.barrier()
        norm_to_h(1)
        K.dma("sp", RhT, h_in.ap().rearrange("(k p) t -> p k t", p=128), hT[:, :, :], r=[RhT], w=[R_hin])
        K.dma("sp", RxT, x1_sp.ap(), xT[:, :, :].rearrange("p k t -> p (k t)"), r=[RxT], w=[R_x1sp])
        K.allgather(ds_cc, h_in.ap().opt(), h_all.ap().opt(), r=[R_hin], w=[R_hall])
        if DEBUG:
            K.dma("pool", ds_x, dbg_h, h_all.ap(), r=[R_hall], w=[])
        K.barrier(skip=(ds_cc,))

    if not SKIP_MIXER:
      with ExitStack() as ph2:
        Wc = sb(ph2, "Wc", [128, KD, NCOLW], BF16)
        R_Wc = Res("Wc")
        win_v = win_d.rearrange("(k p) f -> p k f", p=128)
        for k4 in range(4):
            K.dma("pool", R_Wc, Wc[:, k4 * 4:(k4 + 1) * 4, :], win_v[:, k4 * 4:(k4 + 1) * 4, :], w=[R_Wc])
        hTg = [sb(ph2, "hTg%d" % i, [128, KD, 512], BF16) for i in range(2)]
        R_hTg = [Res("hTg0"), Res("hTg1")]
        tabs = [sb(ph2, "tabs%d" % i, [128, 2, 512], F32) for i in range(2)]
        R_tabs = [Res("tabs0"), Res("tabs1")]

        def fm(name, dt=F32):
            return sb(ph2, name, [128, 512], dt)

        def tm64(name, dt=F32):
            return sb(ph2, name, [64, 8, 64], dt)

        def tm128(name, dt=F32):
            return sb(ph2, name, [64, 8, 128], dt)

        rq, rk, rv = fm("rq", BF16), fm("rk", BF16), fm("rv", BF16)
        rg, tA, tC, tD = fm("rg"), fm("tA"), fm("tC"), fm("tD")
        gq, gk, gv = fm("gq"), fm("gk"), fm("gv")
        gqb = fm("gqb", BF16)
        gkbB = [fm("gkb0", BF16), fm("gkb1", BF16)]
        gvbB = [fm("gvb0", BF16), fm("gvb1", BF16)]
        identb = sb(ph2, "identb", [128, 128], BF16)
        ggb = [fm("gg0"), fm("gg1"), fm("gg2")]
        qdB = [fm("qd0"), fm("qd1")]
        egc, Qeff, rowsb, tB = fm("egc"), fm("Qeff"), fm("rowsb"), fm("tB")
        orT, ogT = fm("orT", BF16), fm("ogT", BF16)
        xpad = [sb(ph2, "xpad%d" % j, [128, 8, 67], F32) for j in range(3)]
        abT = sb(ph2, "abT", [2, 512], F32)
        scT, E1, E2, x1t = tm64("scT", BF16), tm64("E1"), tm64("E2"), tm64("x1t")
        Ug = tm64("Ug")
        NmB = [tm64("Nm0"), tm64("Nm1")]
        NTmB = [tm64("NTm0"), tm64("NTm1")]
        TT0B = [tm64("TT00"), tm64("TT01")]
        QKdTB = [tm64("QKdT0", BF16), tm64("QKdT1", BF16)]
        TTbf = tm64("TTbf", BF16)
        Pb = [tm64("Pa"), tm64("Pb")]
        PTb = [tm64("PTa"), tm64("PTb")]
        TTb = [tm64("TTa"), tm64("TTb")]
        vR, kR, Kb, kd, Vb, Wm, Um, on = (tm128("vR", BF16), tm128("kR", BF16), tm128("Kb", BF16), tm128("kd", BF16),
                                          tm128("Vb", BF16), tm128("Wm", BF16), tm128("Um", BF16), tm128("on", BF16))
        osq = tm128("osq")
        Srb = [sb(ph2, "Srb%d" % i, [128, 128], BF16) for i in range(2)]
        sSrb = sb(ph2, "sSrb", [128, 8, 128], BF16)
        GT = sb(ph2, "GT", [128, 8, 128], F32)
        smB = [sb(ph2, "small%d" % i, [128, 16, 8], F32) for i in range(2)]
        abtm = sb(ph2, "abtm", [64, 8, 2], F32)
        smh = sb(ph2, "smh", [64, 4, 8], F32)
        Sr = [sb(ph2, "Sr%d" % i, [128, 128], F32) for i in range(2)]
        Sg = [sb(ph2, "Sg%d" % i, [128, 128], F32) for i in range(2)]
        sSr = sb(ph2, "sSr", [128, 8, 128], F32)
        sSg = sb(ph2, "sSg", [128, 8, 128], F32)
        cs_in = sb(ph2, "cs_in", [128, 3, 8, 3], F32)
        cout = sb(ph2, "cout", [128, 3, 8, 3], F32)
        carry = sb(ph2, "carry", [128, 3, 3], F32)
        _names = ["rq", "rk", "rv", "rg", "tA", "tB", "gq", "gk", "gv", "gg0", "gg1", "qd", "egc", "Qeff",
                                 "rowsb", "orT", "ogT", "xpad0", "xpad1", "xpad2", "abT", "scT", "E1", "E2", "x1t",
                                 "Nm", "NTm", "QKdT", "Ug", "P0", "P1", "PT0", "PT1", "TT0", "TT1", "vR", "kR", "Kb",
                                 "kd", "Vb", "Wm", "Um", "on", "osq", "GT", "sm", "abtm", "Sr0", "Sr1", "Sg0", "Sg1",
                                 "sSr", "sSg", "cs_in", "cout", "carry", "tC", "gqb", "gkb", "gvb", "TTbf",
                                 "Srb0", "Srb1", "sSrb", "smh"]
        _names += ["gg2", "TT0p", "tD"]
        _dbl = ["Nm", "NTm", "TT0p", "gkb", "gvb", "sm", "qd", "QKdT"]
        _shared = {n: Res(n) for n in _names if n not in _dbl}
        Rpar = [dict(_shared), dict(_shared)]
        for p_ in range(2):
            for n in _dbl:
                Rpar[p_][n] = Res("%s_%d" % (n, p_))
        R = Rpar[0]
        (I_g, I_gc, I_ngc, I_gcb, I_L, I_beta, I_bg, I_gl, I_ss, I_rstd, I_ra, I_cd, I_xa, I_eb, I_t) = range(15)

        K.op("pool", lambda e: e.memset(Sr[0][:, :], 0.0), w=[R["Sr0"]])
        K.op("pool", lambda e: e.memset(Srb[0][:, :], 0.0), w=[R["Srb0"]])
        K.op("dve", lambda e: e.tensor_copy(out=identb[:, :], in_=ident), r=[R_c], w=[R_c])

        def psb(b):
            return PS[b][:, :].bitcast(BF16)
        K.op("pool", lambda e: e.memset(Sg[0][:, :], 0.0), w=[R["Sg0"]])
        K.op("pool", lambda e: e.memset(carry[:, :, :], 0.0), w=[R["carry"]])
        cur = {"r": 0, "g": 0}

        groups = []
        for g in range(16):
            groups.append(dict(kind="p", pieces=[(g // 2, (g % 2) * 512, 512)], tcol=g * 512, idx=g))
        for sgi in range(2):
            groups.append(dict(kind="s", pieces=[(4 * sgi + j, 1024, 128) for j in range(4)], tcol=8192, idx=sgi))

        def load_group(gi):
            G = groups[gi]
            s = gi % 2
            off = 0
            for (rk_, c0, n) in G["pieces"]:
                src = h_all.ap()[rk_ * D:(rk_ + 1) * D, c0:c0 + n].rearrange("(k p) t -> p k t", p=128)
                K.dma("sp", R_hTg[s], hTg[s][:, :, off:off + n], src, r=[R_hall], w=[R_hTg[s]])
                off += n
            K.dma("sp", R_tabs[s], tabs[s][:, :, :], tabs_d[:, :, G["tcol"]:G["tcol"] + 512], w=[R_tabs[s]])

        def c64(i):
            return slice(i * 64, (i + 1) * 64)

        def c128(i4):
            return slice(i4 * 128, (i4 + 1) * 128)

        def v3(ap, a):
            return ap.rearrange("p (a t) -> p a t", a=a)

        dbg_list = []

        def dump(gi_, name, ap, shape, Rt):
            if not DEBUG or gi_ != DBG_GI:
                return
            d = dout("dbg_" + name, shape)
            rr = Res("dbg_" + name)
            K.dma("sp", rr, d, ap, r=[Rt], w=[])

        def make_units(gj):
            sj = gj % 2
            Hj, RHj, TBj, RTBj = hTg[sj], R_hTg[sj], tabs[sj], R_tabs[sj]
            cosj, sinj = TBj[:, 0, :], TBj[:, 1, :]
            ggj, Rggj = ggb[gj % 3], R["gg%d" % (gj % 3)]

            def proj(j, bank, M=128, c0=None):
                c0 = j * 128 if c0 is None else c0
                for k in range(KD):
                    K.op("pe", lambda e: e.matmul(PS[bank][0:M, :], Wc[:, k, c0:c0 + M], Hj[:, k, :],
                                                  start=(k == 0), stop=(k == KD - 1)), r=[R_Wc, RHj], w=[RPS[bank]])

            def rot(j, dst, Rdst):
                proj(j, 0)
                proj(j + 1, 1)
                K.op("dve", lambda e: e.tensor_tensor(out=tC[:, :], in0=PS[0][:, :], in1=cosj, op=ALU.mult),
                     r=[RPS[0], RTBj], w=[R["tC"]])
                K.op("dve", lambda e: e.tensor_tensor(out=tD[:, :], in0=PS[1][:, :], in1=sinj, op=ALU.mult),
                     r=[RPS[1], RTBj], w=[R["tD"]])
                K.op("pool", lambda e: e.tensor_tensor(out=dst[:, :], in0=tC[:, :], in1=tD[:, :], op=ALU.add),
                     r=[R["tD"], R["tC"]], w=[Rdst])

            def single(j, bank, fn):
                def u():
                    proj(j, bank)
                    fn(bank)
                return u

            def u_ab():
                proj(10, 2, M=2, c0=1280)
                K.op("act", lambda e: e.activation(out=abT[:, :], in_=PS[2][0:2, :], func=AF.Copy), r=[RPS[2]],
                     w=[R["abT"]])

            def xp(j):
                if j >= 1:
                    return lambda b: K.op("dve", lambda e: e.tensor_copy(out=xpad[j][:, :, 3:67], in_=v3(PS[b][:, :], 8)),
                                          r=[RPS[b]], w=[R["xpad%d" % j]])
                return lambda b: K.op("act", lambda e: e.activation(out=xpad[j][:, :, 3:67], in_=v3(PS[b][:, :], 8),
                                                                    func=AF.Copy), r=[RPS[b]], w=[R["xpad%d" % j]])
            return [
                u_ab,
                single(6, 4, xp(0)),
                single(7, 2, xp(1)),
                single(8, 3, xp(2)),
                single(9, 4, lambda b: K.op("act", lambda e: e.activation(out=ggj[:, :], in_=PS[b][:, :], func=AF.Silu),
                                            r=[RPS[b]], w=[Rggj])),
                single(4, 2, lambda b: K.op("act", lambda e: e.activation(out=rv[:, :], in_=PS[b][:, :], func=AF.Copy),
                                            r=[RPS[b]], w=[R["rv"]])),
                single(5, 3, lambda b: K.op("act", lambda e: e.activation(out=rg[:, :], in_=PS[b][:, :], func=AF.Silu),
                                            r=[RPS[b]], w=[R["rg"]])),
                lambda: rot(0, rq, R["rq"]),
                lambda: rot(2, rk, R["rk"]),
            ]

        load_group(0)
        def group_gens(gi):
            G = groups[gi]
            is_p = G["kind"] == "p"
            par = gi % 2
            R = Rpar[par]
            Nm, NTm, TT0, gkb, gvb = NmB[par], NTmB[par], TT0B[par], gkbB[par], gvbB[par]
            sm, qd, QKdT = smB[par], qdB[par], QKdTB[par]

            def smv(idx, P=64):
                return sm[0:P, idx, :]
            sg0 = 8 * G["idx"] if not is_p else 0

            def a_prologue():
                if not is_p:
                    K.dma("sp", R["sSr"], sSr[:, :, :], sret_d[sg0:sg0 + 8].rearrange("s d e -> d s e"), w=[R["sSr"]])
                    K.dma("sp", R["cs_in"], cs_in[:, :, :, :], sconv_d[:, :, sg0:sg0 + 8, :], w=[R["cs_in"]])
            gg = ggb[gi % 3]
            Rgg = R["gg%d" % (gi % 3)]

            def head_norm(banks, scale_ap, mult_ap, extra):
                for half in range(2):
                    b = banks[half]
                    if scale_ap is None:
                        K.op("act", lambda e: e.activation(out=osq[:, half * 4:(half + 1) * 4, :],
                                                           in_=v3(PS[b][0:64, :], 4), func=AF.Square),
                             r=[RPS[b]], w=[R["osq"]])
                    else:
                        K.op("act", lambda e: e.activation(out=osq[:, half * 4:(half + 1) * 4, :],
                                                           in_=v3(PS[b][0:64, :], 4), func=AF.Square, scale=scale_ap),
                             r=[RPS[b], R_c], w=[R["osq"]])
                Rh = R["smh"]
                K.op("dve", lambda e: e.tensor_reduce(out=smh[:, 0, :], in_=osq[:, :, :], axis=AX.X, op=ALU.add),
                     r=[R["osq"]], w=[Rh])
                K.op("act", lambda e: e.activation(out=smh[:, 1, :], in_=smh[:, 0, :], func=AF.Ln, bias=hpx[0:64, 2:3],
                                                   scale=1.0), r=[R_c], w=[Rh])
                K.op("act", lambda e: e.activation(out=smh[:, 1, :], in_=smh[:, 1, :], func=AF.Exp, scale=-0.5),
                     r=[], w=[Rh])
                if mult_ap is None:
                    K.op("dve", lambda e: e.tensor_scalar(out=smh[:, 2, :], in0=smh[:, 1, :],
                                                          scalar1=float(np.sqrt(128.0)), scalar2=None, op0=ALU.mult),
                         r=[], w=[Rh])
                else:
                    K.op("dve", lambda e: e.tensor_scalar(out=smh[:, 2, :], in0=smh[:, 1, :], scalar1=mult_ap,
                                                          scalar2=None, op0=ALU.mult), r=[R_c], w=[Rh])
                for half in range(2):
                    b = banks[half]
                    K.op("dve", lambda e: e.tensor_tensor(out=on[:, half * 4:(half + 1) * 4, :],
                                                          in0=v3(PS[b][0:64, :], 4),
                                                          in1=bc(smh[:, 2, half * 4:(half + 1) * 4], [64, 4, 128], 2),
                                                          op=ALU.mult), r=[RPS[b], Rh], w=[R["on"]])
                for i in range(8):
                    K.op("pe", lambda e: e.transpose(psb(0)[:, c64(i)], on[:, i, :], identb[0:64, 0:64]),
                         r=[R["on"], R_c], w=[RPS[0]])
                extra()

            def store_o(tile, Rt, half):
                off = 0
                for (rk_, c0, n) in G["pieces"]:
                    r0 = rk_ * 256 + half * 128
                    K.dma("sp", Rt, o_in.ap()[r0:r0 + 128, c0:c0 + n], tile[:, off:off + n], r=[Rt], w=[R_oin])
                    off += n

            def fin_ret():
                K.op("dve", lambda e: e.scalar_tensor_tensor(out=orT[:, :], in0=psb(0)[:, 0:512], scalar=hp[:, 0:1],
                                                             in1=rg[:, :], op0=ALU.mult, op1=ALU.mult),
                     r=[RPS[0], R_c, R["rg"]], w=[R["orT"]])
                store_o(orT, R["orT"], 0)


            Rsm = R["sm"]

            def ret_gen():
                for i in range(8):
                    K.op("pe", lambda e: e.matmul(PS[7][0:64, c64(i)], rk[:, c64(i)], rq[:, c64(i)], start=True, stop=True),
                         r=[R["rk"], R["rq"]], w=[RPS[7]])
                yield
                K.op("dve", lambda e: e.tensor_tensor(out=scT[:, :, :], in0=v3(PS[7][0:64, :], 8),
                                                      in1=bc(maskR, [64, 8, 64], 1), op=ALU.mult),
                     r=[RPS[7], R_c], w=[R["scT"]])
                yield
                for half in range(2):
                    b1, b2 = 5, 6
                    for i4 in range(4):
                        i = half * 4 + i4
                        K.op("pe", lambda e: e.transpose(psb(b1)[0:64, c128(i4)], rv[:, c64(i)], identb[:, :]),
                             r=[R["rv"], R_c], w=[RPS[b1]])
                    for i4 in range(4):
                        i = half * 4 + i4
                        K.op("pe", lambda e: e.transpose(psb(b2)[0:64, c128(i4)], rk[:, c64(i)], identb[:, :]),
                             r=[R["rk"], R_c], w=[RPS[b2]])
                    K.op("act", lambda e: e.activation(out=vR[:, half * 4:(half + 1) * 4, :],
                                                       in_=v3(psb(b1)[0:64, 0:512], 4),
                                                       func=AF.Copy), r=[RPS[b1]], w=[R["vR"]])
                    K.op("dve", lambda e: e.tensor_scalar(out=kR[:, half * 4:(half + 1) * 4, :],
                                                          in0=v3(psb(b2)[0:64, 0:512], 4),
                                                          scalar1=c_kd, scalar2=None, op0=ALU.mult),
                         r=[RPS[b2], R_c], w=[R["kR"]])
                yield
                if not is_p:
                    K.op("act", lambda e: e.activation(out=sSrb[:, :, :], in_=sSr[:, :, :], func=AF.Copy),
                         r=[R["sSr"]], w=[R["sSrb"]])
                yield
                for i in range(8):
                    ob = 5 + i // 4
                    if is_p:
                        S_in, RS_in = Sr[cur["r"]], R["Sr%d" % cur["r"]]
                        S_out, RS_out = Sr[1 - cur["r"]], R["Sr%d" % (1 - cur["r"])]
                        S_in_ap, S_out_ap = S_in[:, :], S_out[:, :]
                        Sb_in_ap, RSb_in = Srb[cur["r"]][:, :], R["Srb%d" % cur["r"]]
                        Sb_out_ap, RSb_out = Srb[1 - cur["r"]][:, :], R["Srb%d" % (1 - cur["r"])]
                        cur["r"] = 1 - cur["r"]
                    else:
                        S_in_ap = S_out_ap = sSr[:, i, :]
                        RS_in = RS_out = R["sSr"]
                        Sb_in_ap, RSb_in = sSrb[:, i, :], R["sSrb"]
                        Sb_out_ap = None
                    K.op("pe", lambda e: e.matmul(PS[ob][0:64, c128(i % 4)], scT[:, i, :], vR[:, i, :], start=True,
                                                  stop=False), r=[R["scT"], R["vR"]], w=[RPS[ob]])
                    K.op("pe", lambda e: e.matmul(PS[ob][0:64, c128(i % 4)], rq[:, c64(i)], Sb_in_ap, start=False,
                                                  stop=True), r=[R["rq"], RSb_in], w=[RPS[ob]])
                    K.op("pe", lambda e: e.matmul(PS[7][:, c128(i % 4)], kR[:, i, :], vR[:, i, :], start=True, stop=True),
                         r=[R["kR"], R["vR"]], w=[RPS[7]])
                    if Sb_out_ap is not None:
                        K.op("dve", lambda e: e.scalar_tensor_tensor(out=Sb_out_ap, in0=S_in_ap, scalar=c_g64,
                                                                     in1=PS[7][:, c128(i % 4)], op0=ALU.mult, op1=ALU.add),
                             r=[RS_in, RPS[7], R_c], w=[RSb_out])
                    K.op("dve", lambda e: e.scalar_tensor_tensor(out=S_out_ap, in0=S_in_ap, scalar=c_g64,
                                                                 in1=PS[7][:, c128(i % 4)], op0=ALU.mult, op1=ALU.add),
                         r=[RS_in, RPS[7], R_c], w=[RS_out])
                    yield

                yield
                head_norm([5, 6], c_a, c_a2, fin_ret)
                if is_p and gi == 15:
                    K.dma("sp", R["Sr%d" % cur["r"]], oret_d[0], Sr[cur["r"]][:, :], r=[R["Sr%d" % cur["r"]]], w=[])
                if not is_p:
                    K.dma("sp", R["sSr"], oret_d[1 + sg0:9 + sg0].rearrange("s d e -> d s e"), sSr[:, :, :], r=[R["sSr"]], w=[])


            pdone = {"A": False, "B": False}

            def prepA_gen():
                yield
                for j in range(3):
                    Rx = R["xpad%d" % j]
                    if is_p:
                        K.op("pool", lambda e: e.tensor_copy(out=xpad[j][:, 0, 0:3], in_=carry[:, j, :]),
                             r=[R["carry"]], w=[Rx])
                        K.op("pool", lambda e: e.tensor_copy(out=xpad[j][:, 1:8, 0:3], in_=xpad[j][:, 0:7, 64:67]),
                             r=[], w=[Rx])
                        K.op("pool", lambda e: e.tensor_copy(out=carry[:, j, :], in_=xpad[j][:, 7, 64:67]),
                             r=[Rx], w=[R["carry"]])
                    else:
                        K.op("pool", lambda e: e.tensor_copy(out=xpad[j][:, :, 0:3], in_=cs_in[:, j, :, :]),
                             r=[R["cs_in"]], w=[Rx])
                    if (not is_p) or gi == 15:
                        K.op("pool", lambda e: e.tensor_copy(out=cout[:, j, :, :], in_=xpad[j][:, :, 64:67]),
                             r=[Rx], w=[R["cout"]])
                yield
                if is_p and gi == 15:
                    K.dma("sp", R["cout"], oconv_d[:, :, 0, :], cout[:, :, 7, :], r=[R["cout"]], w=[])
                yield
                if not is_p:
                    K.dma("sp", R["cout"], oconv_d[:, :, 1 + sg0:9 + sg0, :], cout[:, :, :, :], r=[R["cout"]], w=[])
                yield
                convd = [gq, gk, gv]
                yield
                Rconv = [R["gq"], R["gk"], R["gv"]]
                yield
                for j in range(3):
                    Rx = R["xpad%d" % j]
                    dst = v3(convd[j][:, :], 8)
                    eng = "dve"
                    K.op(eng, lambda e: e.tensor_scalar(out=dst, in0=xpad[j][:, :, 0:64], scalar1=hp[:, 4 + 4 * j:5 + 4 * j],
                                                        scalar2=None, op0=ALU.mult), r=[Rx, R_c], w=[Rconv[j]])
                    for tp in range(1, 4):
                        K.op(eng, lambda e: e.scalar_tensor_tensor(out=dst, in0=xpad[j][:, :, tp:tp + 64],
                                                                   scalar=hp[:, 4 + 4 * j + tp:5 + 4 * j + tp], in1=dst,
                                                                   op0=ALU.mult, op1=ALU.add), r=[Rx, R_c], w=[Rconv[j]])
                    if j == 2:
                        K.op("act", lambda e: e.activation(out=gvb[:, :], in_=gv[:, :], func=AF.Silu),
                             r=[Rconv[j]], w=[R["gvb"]])
                    else:
                        K.op("act", lambda e: e.activation(out=convd[j][:, :], in_=convd[j][:, :], func=AF.Silu),
                             r=[], w=[Rconv[j]])
                yield
                for j, (t_, Rt_, scl) in enumerate([(gq, R["gq"], float(128.0 ** -0.5)), (gk, R["gk"], 1.0)]):
                    K.op("act", lambda e: e.activation(out=tA[:, :], in_=t_[:, :], func=AF.Square), r=[Rt_], w=[R["tA"]])
                    K.op("pe", lambda e: e.matmul(PS[4][:, :], ones, tA[:, :], start=True, stop=True),
                         r=[R["tA"], R_c], w=[RPS[4]])
                    K.op("act", lambda e: e.activation(out=tB[:, :], in_=PS[4][:, :], func=AF.Ln, bias=eps_ap(128),
                                                       scale=1.0), r=[RPS[4], R_c], w=[R["tB"]])
                    K.op("act", lambda e: e.activation(out=tB[:, :], in_=tB[:, :], func=AF.Exp, scale=-0.5), r=[],
                         w=[R["tB"]])
                    if j == 0:
                        K.op("dve", lambda e: e.scalar_tensor_tensor(out=gq[:, :], in0=gq[:, :], scalar=scl, in1=tB[:, :],
                                                                     op0=ALU.mult, op1=ALU.mult), r=[R["tB"]], w=[Rt_])
                        K.op("pool", lambda e: e.tensor_copy(out=gqb[:, :], in_=gq[:, :]), r=[Rt_], w=[R["gqb"]])
                    else:
                        K.op("dve", lambda e: e.scalar_tensor_tensor(out=gkb[:, :], in0=gk[:, :], scalar=scl, in1=tB[:, :],
                                                                     op0=ALU.mult, op1=ALU.mult), r=[R["tB"], Rt_],
                             w=[R["gkb"]])
                pdone["A"] = True

            def prepB_gen():
                yield
                for i in range(8):
                    K.op("pe", lambda e: e.transpose(PS[0][0:64, 2 * i:2 * i + 2], abT[0:2, c64(i)], ident[0:2, 0:2]),
                         r=[R["abT"], R_c], w=[RPS[0]])
                yield
                K.op("act", lambda e: e.activation(out=abtm[:, :, :], in_=PS[0][0:64, 0:16].rearrange("p (i c) -> p i c", c=2),
                                                   func=AF.Copy), r=[RPS[0]], w=[R["abtm"]])
                yield
                Rsm = R["sm"]
                yield
                K.op("act", lambda e: e.activation(out=smv(I_xa), in_=abtm[:, :, 0], func=AF.Exp, bias=hp[0:64, 3:4],
                                                   scale=1.0), r=[R["abtm"], R_c], w=[Rsm])
                yield
                K.op("act", lambda e: e.activation(out=smv(I_xa), in_=smv(I_xa), func=AF.Ln, bias=1.0, scale=1.0),
                     r=[], w=[Rsm])
                yield
                K.op("dve", lambda e: e.tensor_scalar(out=smv(I_g), in0=smv(I_xa), scalar1=negA, scalar2=None,
                                                      op0=ALU.mult), r=[R_c], w=[Rsm])
                yield
                K.op("act", lambda e: e.activation(out=smv(I_eb), in_=abtm[:, :, 1], func=AF.Exp, scale=-1.0),
                     r=[R["abtm"]], w=[Rsm])
                yield
                K.op("act", lambda e: e.activation(out=smv(I_L), in_=smv(I_eb), func=AF.Ln, bias=1.0, scale=1.0),
                     r=[], w=[Rsm])
                yield
                K.op("dve", lambda e: e.tensor_scalar(out=smv(I_t), in0=smv(I_eb), scalar1=1.0, scalar2=None, op0=ALU.add),
                     r=[], w=[Rsm])
                yield
                K.op("dve", lambda e: e.reciprocal(out=smv(I_beta), in_=smv(I_t)), r=[], w=[Rsm])
                yield
                K.op("pe", lambda e: e.matmul(PS[0][0:64, 0:8], Umat, smv(I_g), start=True, stop=True),
                     r=[Rsm, R_c], w=[RPS[0]])
                yield
                K.op("act", lambda e: e.activation(out=smv(I_gc), in_=PS[0][0:64, 0:8], func=AF.Copy), r=[RPS[0]], w=[Rsm])
                yield
                K.op("dve", lambda e: e.tensor_tensor(out=smv(I_gcb), in0=smv(I_gc), in1=smv(I_L), op=ALU.subtract),
                     r=[], w=[Rsm])
                yield
                K.op("dve", lambda e: e.tensor_tensor(out=Ug[:, :, :], in0=bc(Umat, [64, 8, 64], 1),
                                                      in1=bc(smv(I_g), [64, 8, 64], 2), op=ALU.mult),
                     r=[R_c, Rsm], w=[R["Ug"]])
                yield
                for i in range(8):
                    K.op("pe", lambda e: e.matmul(PS[0][:, c64(i)], ones[0:64, :], Ug[:, i, :], start=True, stop=True),
                         r=[R["Ug"], R_c], w=[RPS[0]])
                yield
                K.op("act", lambda e: e.activation(out=rowsb[:, :], in_=PS[0][:, :], func=AF.Copy), r=[RPS[0]],
                     w=[R["rowsb"]])
                yield
                K.op("act", lambda e: e.activation(out=egc[:, :], in_=PS[0][:, :], func=AF.Exp), r=[RPS[0]], w=[R["egc"]])
                yield
                row3 = v3(rowsb[:, :], 8)
                yield
                K.op("act", lambda e: e.activation(out=sm[:, I_cd, :], in_=row3[:, :, 63], func=AF.Exp),
                     r=[R["rowsb"]], w=[Rsm])
                yield
                K.op("dve", lambda e: e.tensor_tensor(out=smv(I_gl), in0=row3[0:64, :, 63], in1=smv(I_gc), op=ALU.subtract),
                     r=[R["rowsb"]], w=[Rsm])
                yield
                K.op("act", lambda e: e.activation(out=smv(I_gl), in_=smv(I_gl), func=AF.Exp), r=[], w=[Rsm])
                yield
                K.op("act", lambda e: e.activation(out=smv(I_bg), in_=smv(I_gcb), func=AF.Exp), r=[], w=[Rsm])
                yield
                K.op("dve", lambda e: e.scalar_tensor_tensor(out=x1t[:, :, :], in0=row3[0:64, :, :], scalar=-1.0,
                                                             in1=bc(NEGs, [64, 8, 64], 1), op0=ALU.mult, op1=ALU.add),
                     r=[R["rowsb"], R_c], w=[R["x1t"]])
                yield
                K.op("dve", lambda e: e.tensor_tensor(out=x1t[:, :, :], in0=x1t[:, :, :], in1=bc(smv(I_gcb), [64, 8, 64], 2),
                                                      op=ALU.add), r=[Rsm], w=[R["x1t"]])
                yield
                K.op("act", lambda e: e.activation(out=E1[:, :, :], in_=x1t[:, :, :], func=AF.Exp), r=[R["x1t"]],
                     w=[R["E1"]])
                yield
                K.op("dve", lambda e: e.tensor_tensor(out=x1t[:, :, :], in0=row3[0:64, :, :], in1=bc(NEGu, [64, 8, 64], 1),
                                                      op=ALU.add), r=[R["rowsb"], R_c], w=[R["x1t"]])
                yield
                K.op("dve", lambda e: e.tensor_tensor(out=x1t[:, :, :], in0=x1t[:, :, :], in1=bc(smv(I_gc), [64, 8, 64], 2),
                                                      op=ALU.subtract), r=[Rsm], w=[R["x1t"]])
                yield
                K.op("act", lambda e: e.activation(out=E2[:, :, :], in_=x1t[:, :, :], func=AF.Exp), r=[R["x1t"]],
                     w=[R["E2"]])
                pdone["B"] = True

            def prep3_gen():
                while not (pdone["A"] and pdone["B"]):
                    yield
                K.op("dve", lambda e: e.tensor_tensor(out=qd[:, :], in0=gq[:, :], in1=egc[:, :], op=ALU.mult),
                     r=[R["gq"], R["egc"]], w=[R["qd"]])
                yield
                yield
                for i in range(8):
                    K.op("pe", lambda e: e.matmul(PS[4][0:64, c64(i)], gkb[:, c64(i)], gkb[:, c64(i)], start=True, stop=True),
                         r=[R["gkb"]], w=[RPS[4]])
                yield
                for i in range(8):
                    K.op("pe", lambda e: e.matmul(PS[0][0:64, c64(i)], gkb[:, c64(i)], gqb[:, c64(i)], start=True, stop=True),
                         r=[R["gkb"], R["gqb"]], w=[RPS[0]])
                yield
                K.op("dve", lambda e: e.scalar_tensor_tensor(out=Nm[:, :, :], in0=v3(PS[4][0:64, :], 8), scalar=-1.0,
                                                             in1=E1[:, :, :], op0=ALU.mult, op1=ALU.mult),
                     r=[RPS[4], R["E1"]], w=[R["Nm"]])
                yield
                K.op("dve", lambda e: e.tensor_tensor(out=QKdT[:, :, :], in0=v3(PS[0][0:64, :], 8), in1=E2[:, :, :],
                                                      op=ALU.mult), r=[RPS[0], R["E2"]], w=[R["QKdT"]])
                yield
                for i in range(8):
                    K.op("pe", lambda e: e.transpose(PS[4][0:64, c64(i)], Nm[:, i, :], ident[0:64, 0:64]),
                         r=[R["Nm"], R_c], w=[RPS[4]])
                yield
                K.op("act", lambda e: e.activation(out=NTm[:, :, :], in_=v3(PS[4][0:64, :], 8), func=AF.Copy),
                     r=[RPS[4]], w=[R["NTm"]])
                yield
                K.op("dve", lambda e: e.tensor_tensor(out=TT0[:, :, :], in0=NTm[:, :, :],
                                                      in1=bc(ident[0:64, 0:64], [64, 8, 64], 1), op=ALU.add),
                     r=[R["NTm"], R_c], w=[R["TT0p"]])
                yield


            def prep_gen():
                gA, gB, g3 = prepA_gen(), prepB_gen(), prep3_gen()
                live = [gA, gB, g3]
                while live:
                    for g_ in list(live):
                        try:
                            next(g_)
                        except StopIteration:
                            live.remove(g_)
                    yield

            def main1_gen():
                if not is_p:
                    K.dma("sp", R["sSg"], sSg[:, :, :], sgdn_d[sg0:sg0 + 8].rearrange("s d e -> d s e"), w=[R["sSg"]])
                P_, RP_, PT_, RPT_ = Nm, R["Nm"], NTm, R["NTm"]
                yield
                tt = 0
                yield
                for lev in range(5):
                    pn, ptn = Pb[lev % 2], PTb[lev % 2]
                    Rpn, Rptn = R["P%d" % (lev % 2)], R["PT%d" % (lev % 2)]
                    for i in range(8):
                        K.op("pe", lambda e: e.matmul(PS[1][0:64, c64(i)], PT_[:, i, :], P_[:, i, :], start=True, stop=True),
                             r=[RP_, RPT_], w=[RPS[1]])
                    if lev < 4:
                        for i in range(8):
                            K.op("pe", lambda e: e.matmul(PS[2][0:64, c64(i)], P_[:, i, :], PT_[:, i, :], start=True,
                                                          stop=True), r=[RP_, RPT_], w=[RPS[2]])
                    K.op("act", lambda e: e.activation(out=pn[:, :, :], in_=v3(PS[1][0:64, :], 8), func=AF.Copy),
                         r=[RPS[1]], w=[Rpn])
                    if lev < 4:
                        K.op("dve", lambda e: e.tensor_copy(out=ptn[:, :, :], in_=v3(PS[2][0:64, :], 8)),
                             r=[RPS[2]], w=[Rptn])
                    if lev == 0:
                        TTc, RTTc = TT0, R["TT0p"]
                    else:
                        TTc, RTTc = TTb[(lev - 1) % 2], R["TT%d" % ((lev - 1) % 2)]
                    TTn, RTTn = TTb[lev % 2], R["TT%d" % (lev % 2)]
                    for i in range(8):
                        K.op("pe", lambda e: e.matmul(PS[3][0:64, c64(i)], pn[:, i, :], TTc[:, i, :], start=True, stop=True),
                             r=[Rpn, RTTc], w=[RPS[3]])
                    if lev == 4:
                        TTn, RTTn = TTbf, R["TTbf"]
                    K.op("dve", lambda e: e.tensor_tensor(out=TTn[:, :, :], in0=v3(PS[3][0:64, :], 8), in1=TTc[:, :, :],
                                                          op=ALU.add), r=[RPS[3], RTTc], w=[RTTn])
                    tt = 1 - tt
                    P_, RP_, PT_, RPT_ = pn, Rpn, ptn, Rptn
                    yield
                yield
                TT, RTT = TTbf, R["TTbf"]
                yield
                for half in range(2):
                    hs = slice(half * 4, (half + 1) * 4)
                    b1, b2 = 1, 2
                    for i4 in range(4):
                        i = half * 4 + i4
                        K.op("pe", lambda e: e.transpose(psb(b1)[0:64, c128(i4)], gkb[:, c64(i)], identb[:, :]),
                             r=[R["gkb"], R_c], w=[RPS[b1]])
                    for i4 in range(4):
                        i = half * 4 + i4
                        K.op("pe", lambda e: e.transpose(psb(b2)[0:64, c128(i4)], gvb[:, c64(i)], identb[:, :]),
                             r=[R["gvb"], R_c], w=[RPS[b2]])
                    K.op("dve", lambda e: e.tensor_tensor(out=Kb[:, hs, :], in0=v3(psb(b1)[0:64, 0:512], 4),
                                                          in1=bc(sm[0:64, I_bg, hs], [64, 4, 128], 2), op=ALU.mult),
                         r=[RPS[b1], Rsm], w=[R["Kb"]])
                    K.op("dve", lambda e: e.tensor_tensor(out=kd[:, hs, :], in0=v3(psb(b1)[0:64, 0:512], 4),
                                                          in1=bc(sm[0:64, I_gl, hs], [64, 4, 128], 2), op=ALU.mult),
                         r=[RPS[b1], Rsm], w=[R["kd"]])
                    K.op("dve", lambda e: e.tensor_tensor(out=Vb[:, hs, :], in0=v3(psb(b2)[0:64, 0:512], 4),
                                                          in1=bc(sm[0:64, I_beta, hs], [64, 4, 128], 2), op=ALU.mult),
                         r=[RPS[b2], Rsm], w=[R["Vb"]])
                yield
                for half in range(2):
                    hs = slice(half * 4, (half + 1) * 4)
                    b1, b2 = 1, 2
                    for i4 in range(4):
                        i = half * 4 + i4
                        K.op("pe", lambda e: e.matmul(PS[b1][0:64, c128(i4)], TT[:, i, :], Kb[:, i, :], start=True,
                                                      stop=True), r=[RTT, R["Kb"]], w=[RPS[b1]])
                    for i4 in range(4):
                        i = half * 4 + i4
                        K.op("pe", lambda e: e.matmul(PS[b2][0:64, c128(i4)], TT[:, i, :], Vb[:, i, :], start=True,
                                                      stop=True), r=[RTT, R["Vb"]], w=[RPS[b2]])
                    K.op("act", lambda e: e.activation(out=Wm[:, hs, :], in_=v3(PS[b1][0:64, :], 4), func=AF.Copy),
                         r=[RPS[b1]], w=[R["Wm"]])
                    K.op("act", lambda e: e.activation(out=Um[:, hs, :], in_=v3(PS[b2][0:64, :], 4), func=AF.Copy),
                         r=[RPS[b2]], w=[R["Um"]])
                yield
                for half in range(2):
                    b1 = 1 + half
                    for i4 in range(4):
                        i = half * 4 + i4
                        K.op("pe", lambda e: e.matmul(PS[b1][:, c128(i4)], Wm[:, i, :], kd[:, i, :], start=True, stop=True),
                             r=[R["Wm"], R["kd"]], w=[RPS[b1]])
                    for i4 in range(4):
                        i = half * 4 + i4
                        K.op("dve", lambda e: e.scalar_tensor_tensor(out=GT[:, i, :], in0=ident, scalar=sm[:, I_cd, i:i + 1],
                                                                     in1=PS[b1][:, c128(i4)], op0=ALU.mult,
                                                                     op1=ALU.subtract),
                             r=[RPS[b1], Rsm, R_c], w=[R["GT"]])
                yield
                for i in range(8):
                    K.op("pe", lambda e: e.matmul(PS[3][:, c64(i)], Wm[:, i, :], QKdT[:, i, :], start=True, stop=True),
                         r=[R["Wm"], R["QKdT"]], w=[RPS[3]])
                yield
                K.op("dve", lambda e: e.tensor_tensor(out=Qeff[:, :], in0=qd[:, :], in1=PS[3][:, :], op=ALU.subtract),
                     r=[R["qd"], RPS[3]], w=[R["Qeff"]])
                yield

            def main2_gen():
                yield
                for i in range(8):
                    ob = 5 + i // 4
                    if is_p:
                        S_in_ap, RS_in = Sg[cur["g"]][:, :], R["Sg%d" % cur["g"]]
                        S_out_ap, RS_out = Sg[1 - cur["g"]][:, :], R["Sg%d" % (1 - cur["g"])]
                        cur["g"] = 1 - cur["g"]
                    else:
                        S_in_ap = S_out_ap = sSg[:, i, :]
                        RS_in = RS_out = R["sSg"]
                    K.op("pe", lambda e: e.matmul(PS[ob][0:64, c128(i % 4)], Qeff[:, c64(i)], S_in_ap, start=True,
                                                  stop=False), r=[R["Qeff"], RS_in], w=[RPS[ob]])
                    K.op("pe", lambda e: e.matmul(PS[ob][0:64, c128(i % 4)], QKdT[:, i, :], Um[:, i, :], start=False,
                                                  stop=True), r=[R["QKdT"], R["Um"]], w=[RPS[ob]])
                    K.op("pe", lambda e: e.matmul(PS[7][:, c128(i % 4)], GT[:, i, :], S_in_ap, start=True, stop=False),
                         r=[R["GT"], RS_in], w=[RPS[7]])
                    K.op("pe", lambda e: e.matmul(PS[7][:, c128(i % 4)], kd[:, i, :], Um[:, i, :], start=False, stop=True),
                         r=[R["kd"], R["Um"]], w=[RPS[7]])
                    K.op("dve", lambda e: e.tensor_copy(out=S_out_ap, in_=PS[7][:, c128(i % 4)]),
                         r=[RPS[7]], w=[RS_out])
                    yield

                yield
                def fin_gdn():
                    K.op("dve", lambda e: e.scalar_tensor_tensor(out=ogT[:, :], in0=psb(0)[:, 0:512], scalar=hp[:, 1:2],
                                                                 in1=gg[:, :], op0=ALU.mult, op1=ALU.mult),
                         r=[RPS[0], R_c, Rgg], w=[R["ogT"]])
                    store_o(ogT, R["ogT"], 1)

                yield
                head_norm([5, 6], None, None, fin_gdn)
                yield
                if is_p and gi == 15:
                    K.dma("sp", R["Sg%d" % cur["g"]], ogdn_d[0], Sg[cur["g"]][:, :], r=[R["Sg%d" % cur["g"]]], w=[])
                yield
                if not is_p:
                    K.dma("sp", R["sSg"], ogdn_d[1 + sg0:9 + sg0].rearrange("s d e -> d s e"), sSg[:, :, :], r=[R["sSg"]], w=[])
                yield

            return dict(a_prologue=a_prologue, ret=ret_gen, prep=prep_gen, main1=main1_gen, main2=main2_gen,
                        prepA=prepA_gen, prepB=prepB_gen, prep3=prep3_gen)

        def run_all(gens, lates=()):
            gens = list(gens)
            lates = sorted(lates, key=lambda t: t[0])
            rounds = 0
            while gens or lates:
                while lates and (rounds >= lates[0][0] or not gens):
                    gens.append(lates.pop(0)[1])
                for g_ in list(gens):
                    try:
                        next(g_)
                    except StopIteration:
                        gens.remove(g_)
                rounds += 1

        def units_gen(gj):
            for u in make_units(gj):
                u()
                yield

        NG = len(groups)
        GG = [group_gens(g) for g in range(NG)]
        load_group(1)
        run_all([units_gen(0)])
        GG[0]["a_prologue"]()
        run_all([GG[0]["prepA"](), GG[0]["prepB"]()])
        run_all([GG[0]["ret"](), GG[0]["prep3"]()])
        GG[1]["a_prologue"]()
        run_all([units_gen(1)], lates=[(1, GG[1]["prepB"]()), (4, GG[1]["prepA"]())])
        for g in range(NG):
            if g + 2 < NG:
                load_group(g + 2)
            gens = [GG[g]["main1"]()]
            if g + 1 < NG:
                gens += [GG[g + 1]["ret"](), GG[g + 1]["prep3"]()]
            run_all(gens)
            gens = [GG[g]["main2"]()]
            lates = []
            if g + 2 < NG:
                GG[g + 2]["a_prologue"]()
                gens.append(units_gen(g + 2))
                lates = [(1, GG[g + 2]["prepB"]()), (4, GG[g + 2]["prepA"]())]
            run_all(gens, lates=lates)
        K.barrier()


    with ExitStack() as ph3:
        xT = sb(ph3, "xT3", [128, KD, TOK], F32)
        RxT = Res("xT3")
        K.dma("sp", RxT, xT[:, :, :].rearrange("p k t -> p (k t)"), x1_sp.ap(), r=[R_x1sp], w=[RxT])
        with ExitStack() as s3:
            oT = sb(s3, "oT", [128, KD, TOK], BF16)
            RoT = Res("oT")
            wo = [sb(s3, "wo%d" % k, [128, D], BF16) for k in range(KD)]
            Rwo = Res("wo")
            for k in range(KD):
                r0 = (k // 2) * 128 + (k % 2) * 1024
                K.dma("pool", Rwo, wo[k][:, :], wout_d[r0:r0 + 128, :], w=[Rwo])
            K.allgather(ds_cc, o_in.ap().opt(), o_all.ap().opt(), r=[R_oin], w=[R_oall])
            if DEBUG:
                K.dma("pool", ds_x, dbg_o, o_in.ap(), r=[R_oin], w=[])
            for k in range(KD):
                K.idma(RoT, oT[:, k, :], o_all.ap(), gidx[:, k:k + 1], r=[R_oall, R_c], w=[RoT])
            if not SKIP_MIXER:
                cnt = 0
                for m in range(KD):
                    for nb in range(NB):
                        cols = slice(nb * BW, (nb + 1) * BW)
                        pb = cnt % 4
                        cnt += 1
                        for k in range(KD):
                            K.op("pe", lambda e: e.matmul(PS[pb][:, 0:BW], wo[k][:, m * 128:(m + 1) * 128], oT[:, k, cols],
                                                          start=(k == 0), stop=(k == KD - 1)), r=[Rwo, RoT], w=[RPS[pb]])
                        K.op("dve", lambda e: e.tensor_tensor(out=xT[:, m, cols], in0=PS[pb][:, 0:BW], in1=xT[:, m, cols],
                                                              op=ALU.add), r=[RPS[pb]], w=[RxT])
            K.barrier()
        hT = sb(ph3, "hT3", [128, KD, TOK], BF16)
        sq = sb(ph3, "sq3", [128, 4, BW], F32)
        rs = sb(ph3, "rs3", [128, BW], F32)
        RhT, Rsq, Rrs = Res("hT3"), Res("sq3"), Res("rs3")

        def out_fn2(nb, k, cols):
            K.op("dve", lambda e: e.scalar_tensor_tensor(out=hT[:, k, cols], in0=xT[:, k, cols],
                                                         scalar=nrm[:, 2 * KD + k:2 * KD + k + 1], in1=rs[:, :],
                                                         op0=ALU.mult, op1=ALU.mult), r=[RxT, Rrs, R_c], w=[RhT])
        rmsnorm_fm(xT, RxT, out_fn2, sq, Rsq, rs, Rrs)
        with ExitStack() as s4:
            ffn(1, xT, RxT, hT, RhT, s4)
            K.barrier()
        def out_fn3(nb, k, cols):
            K.op("dve", lambda e: e.scalar_tensor_tensor(out=xT[:, k, cols], in0=xT[:, k, cols],
                                                         scalar=nrm[:, 3 * KD + k:3 * KD + k + 1], in1=rs[:, :],
                                                         op0=ALU.mult, op1=ALU.mult), r=[Rrs, R_c], w=[RxT])
        rmsnorm_fm(xT, RxT, out_fn3, sq, Rsq, rs, Rrs)
        with ExitStack() as s5:
            stg = [sb(s5, "ystg%d" % i, [128, D], F32) for i in range(2)]
            Rstg = [Res("ystg0"), Res("ystg1")]
            for i in range(TOK // 128):
                s = i % 2
                for k4 in range(4):
                    b = k4 % 2
                    for kk in range(4):
                        k = k4 * 4 + kk
                        K.op("pe", lambda e: e.transpose(PS[b][:, kk * 128:(kk + 1) * 128],
                                                         xT[:, k, i * 128:(i + 1) * 128], ident),
                             r=[RxT, R_c], w=[RPS[b]])
                    dst = stg[s][:, k4 * 512:(k4 + 1) * 512]
                    if k4 % 2 == 0:
                        K.op("act", lambda e: e.activation(out=dst, in_=PS[b][:, :], func=AF.Copy), r=[RPS[b]],
                             w=[Rstg[s]])
                    else:
                        K.op("dve", lambda e: e.tensor_copy(out=dst, in_=PS[b][:, :]), r=[RPS[b]], w=[Rstg[s]])
                K.dma("sp", Rstg[s], y_d[i * 128:(i + 1) * 128, :], stg[s][:, :], r=[Rstg[s]], w=[])
            K.final_wait("sp")
            K.barrier()
    return nc


def _consts(c):
    gam = 1.0 - 2.0 ** (-5.0 - c)
    cst = np.zeros((128, CSTW), np.float64)
    cst[:, 0:128] = np.eye(128)
    cst[:, 128:256] = 1.0
    t = np.arange(64)
    s_, t_ = np.meshgrid(t, t, indexing="ij")
    cst[0:64, 256:320] = np.where(t_ >= s_, gam ** (-(s_ + 1.0)), 0.0)
    cst[0:64, 320:384] = (s_ <= t_).astype(np.float64)
    cst[0:64, 384:448] = np.where(s_ > t_, 0.0, NEG)
    cst[0:64, 448:512] = np.where(t_ >= s_, 0.0, NEG)
    tt = np.arange(128) % 64
    cst[:, 512] = gam ** (tt + 1.0) / np.sqrt(128.0)
    cst[:, 513] = gam ** (tt + 1.0)
    cst[:, 514] = gam ** (63.0 - tt)
    cst[:, 515] = gam ** 64.0
    return cst.astype(np.float32)


def _tabs():
    half = 64
    inv = (10000.0 ** (-np.arange(half, dtype=np.float32) / half)).astype(np.float32)
    pos = np.concatenate([np.arange(8192, dtype=np.float32),
                          np.tile(4096.0 + np.arange(64, dtype=np.float32), 8)]).astype(np.float32)
    ang = (pos[None, :] * inv[:, None]).astype(np.float32).astype(np.float64)
    cos, sin = np.cos(ang), np.sin(ang)
    tabs = np.zeros((128, 2, NPOS), np.float32)
    tabs[0:64, 0] = cos
    tabs[64:128, 0] = cos
    tabs[0:64, 1] = -sin
    tabs[64:128, 1] = sin
    return tabs


_CACHE = {}


def kernel(x_prompt, x_sample, state_ret, state_gdn, state_conv, ffn1_norm, ffn1_w_gate, ffn1_w_up,
           ffn1_w_down, mix_norm, w_in, ret_norm, gdn_conv, gdn_a_log, gdn_dt_bias, gdn_norm, w_out,
           ffn2_norm, ffn2_w_gate, ffn2_w_up, ffn2_w_down, final_norm):
    f32 = np.float32
    A = lambda a: np.ascontiguousarray(np.asarray(a, dtype=f32))
    x_prompt, x_sample = A(x_prompt), A(x_sample)
    w_in0, w_out0 = A(w_in)[0], A(w_out)[0]
    if "nc" not in _CACHE:
        _CACHE["nc"] = build_program()
    nc = _CACHE["nc"]
    nrm = np.stack([A(ffn1_norm)[0], A(mix_norm)[0], A(ffn2_norm)[0], A(final_norm)])
    nrm = np.ascontiguousarray(nrm.reshape(4, KD, 128).transpose(2, 0, 1).reshape(128, 4 * KD))
    tabs = _tabs()
    shared = dict(nrm=nrm, wg1=A(ffn1_w_gate)[0], wu1=A(ffn1_w_up)[0], wd1=A(ffn1_w_down)[0],
                  wg2=A(ffn2_w_gate)[0], wu2=A(ffn2_w_up)[0], wd2=A(ffn2_w_down)[0], w_out=w_out0, tabs=tabs)
    sr, sgd, sc = A(state_ret)[0], A(state_gdn)[0], A(state_conv)[0]
    cw = A(gdn_conv)[0]
    swap = np.concatenate([np.arange(64, 128), np.arange(0, 64)])
    in_maps = []
    for c in range(NCORES):
        x_tok = np.concatenate([x_prompt[0, 1024 * c:1024 * (c + 1)], x_sample[2 * c:2 * c + 2].reshape(128, D)], 0)
        hc = slice(c * 128, (c + 1) * 128)
        o_rq, o_rk, o_rv, o_rg, o_g = 0, 1024, 2048, 3072, 4096
        qcols = o_g + c * 128
        kcols = o_g + 1024 + c * 128
        vcols = o_g + 2048 + c * 128
        ggcols = o_g + 3072 + c * 128
        acol = o_g + 4096 + c
        bcol = o_g + 4096 + 8 + c
        rqc = w_in0[:, o_rq + c * 128:o_rq + (c + 1) * 128]
        rkc = w_in0[:, o_rk + c * 128:o_rk + (c + 1) * 128]
        wc = np.concatenate([rqc, rqc[:, swap], rkc, rkc[:, swap],
                             w_in0[:, o_rv + c * 128:o_rv + (c + 1) * 128],
                             w_in0[:, o_rg + c * 128:o_rg + (c + 1) * 128],
                             w_in0[:, qcols:qcols + 128], w_in0[:, kcols:kcols + 128], w_in0[:, vcols:vcols + 128],
                             w_in0[:, ggcols:ggcols + 128], w_in0[:, acol:acol + 1], w_in0[:, bcol:bcol + 1]], 1)
        hpv = np.zeros((128, 16), f32)
        hpv[:, 0] = A(ret_norm)[0][hc]
        hpv[:, 1] = A(gdn_norm)[0]
        hpv[:, 2] = A(gdn_a_log)[0][c]
        hpv[:, 3] = A(gdn_dt_bias)[0][c]
        for j in range(3):
            for tp in range(4):
                hpv[:, 4 + 4 * j + tp] = cw[tp, j * 1024 + c * 128:j * 1024 + (c + 1) * 128]
        scv = np.stack([sc[:, :, j * 1024 + c * 128:j * 1024 + (c + 1) * 128] for j in range(3)], 0)
        scv = np.ascontiguousarray(scv.transpose(3, 0, 1, 2))
        gix = np.zeros((128, KD), np.int32)
        for k in range(KD):
            src_c, half = k // 2, k % 2
            gix[:, k] = src_c * (NCORES * 256) + c * 256 + half * 128 + np.arange(128)
        m = dict(shared)
        m.update(x_tok=np.ascontiguousarray(x_tok), w_in_c=np.ascontiguousarray(wc), cst=_consts(c),
                 s_ret=np.ascontiguousarray(sr[:, c]), s_gdn=np.ascontiguousarray(sgd[:, c]), s_conv=scv, hp=hpv,
                 gidx=gix)
        in_maps.append(m)
    res = run_bass_kernel_spmd(nc, in_maps, core_ids=list(range(NCORES)))
    R_ = res.results
    _CACHE["last"] = R_
    y_prompt = np.zeros((1, 8192, D), f32)
    y_sample = np.zeros((16, 64, D), f32)
    ret_p = np.zeros((1, 1, 8, 128, 128), f32)
    gdn_p = np.zeros((1, 1, 8, 128, 128), f32)
    conv_p = np.zeros((1, 1, 3, 3072), f32)
    ret_s = np.zeros((1, 16, 8, 128, 128), f32)
    gdn_s = np.zeros((1, 16, 8, 128, 128), f32)
    conv_s = np.zeros((1, 16, 3, 3072), f32)
    for c in range(NCORES):
        y = np.asarray(R_[c]["y"], f32)
        y_prompt[0, 1024 * c:1024 * (c + 1)] = y[:1024]
        y_sample[2 * c:2 * c + 2] = y[1024:].reshape(2, 64, D)
        sr_o, sg_o = np.asarray(R_[c]["st_ret"], f32), np.asarray(R_[c]["st_gdn"], f32)
        ret_p[0, 0, c], gdn_p[0, 0, c] = sr_o[0], sg_o[0]
        ret_s[0, :, c], gdn_s[0, :, c] = sr_o[1:], sg_o[1:]
        cv = np.asarray(R_[c]["st_conv"], f32)
        for j in range(3):
            cs_ = slice(j * 1024 + c * 128, j * 1024 + (c + 1) * 128)
            conv_p[0, 0, :, cs_] = cv[:, j, 0, :].T
            conv_s[0, :, :, cs_] = cv[:, j, 1:, :].transpose(1, 2, 0)
    return (y_prompt, y_sample, ret_p, gdn_p, conv_p, ret_s, gdn_s, conv_s)
```
